# Optimizing a Trainium2 kernel written in Bass

```python
import jax, jax.numpy as jnp
from jax import lax
import numpy as np

D_MODEL = 1024
BATCH = 4
SEQ = 4096
DEPTH = 2

N_A = DEPTH // 2
N_B = DEPTH - N_A

LRU_WIDTH = D_MODEL
LRU_HEADS = 8
LRU_BLOCK = LRU_WIDTH // LRU_HEADS
LRU_CONV = 4
LRU_C = 8.0

N_HEADS = 8
HEAD_DIM = D_MODEL // N_HEADS
Q_BLOCK = 128

D_FF = 3 * D_MODEL
FFN_CONV = 3

DN_ALPHA = (2 * DEPTH) ** 0.25
DN_BETA = (8 * DEPTH) ** -0.25
LN_EPS = 1e-5

kernel_name = "yoco_rglru_stickbreak_convffn_deepnorm"


def layer_norm(x, g, b):
    xf = x.astype(jnp.float32)
    mu = jnp.mean(xf, axis=-1, keepdims=True)
    xc = xf - mu
    var = jnp.mean(xc * xc, axis=-1, keepdims=True)
    y = xc * lax.rsqrt(var + LN_EPS) * g.astype(jnp.float32) + b.astype(jnp.float32)
    return y.astype(x.dtype)


def causal_depthwise_conv(x, w, b):
    K = w.shape[0]
    S = x.shape[1]
    xp = jnp.pad(x, ((0, 0), (K - 1, 0), (0, 0)))
    y = b + xp[:, 0:S] * w[0]
    for k in range(1, K):
        y = y + xp[:, k:k + S] * w[k]
    return y


def _lin_combine(left, right):
    a1, b1 = left
    a2, b2 = right
    return a1 * a2, a2 * b1 + b2


def rg_lru_block(x, w_in, b_in, conv_w, conv_b, w_gates, b_gates, a_param, w_out, b_out):
    B, S, _ = x.shape
    proj = x @ w_in + b_in
    y_br = jax.nn.gelu(proj[..., :LRU_WIDTH])
    x_br = causal_depthwise_conv(proj[..., LRU_WIDTH:], conv_w, conv_b)
    xb = x_br.reshape(B, S, LRU_HEADS, LRU_BLOCK)
    gates = jnp.einsum('bsnc,ncg->bsng', xb, w_gates) + b_gates
    gates = jax.nn.sigmoid(gates.astype(jnp.float32))
    gate_i = gates[..., :LRU_BLOCK].reshape(B, S, LRU_WIDTH)
    gate_r = gates[..., LRU_BLOCK:].reshape(B, S, LRU_WIDTH)
    log_a = -LRU_C * gate_r * jax.nn.softplus(-a_param.astype(jnp.float32))
    a = jnp.exp(log_a)
    mult = jnp.sqrt(jnp.maximum(1.0 - jnp.exp(2.0 * log_a), 0.0))
    is_start = (jnp.arange(S) == 0)[None, :, None]
    mult = jnp.where(is_start, 1.0, mult)
    u = mult * gate_i * x_br.astype(jnp.float32)
    _, h = lax.associative_scan(_lin_combine, (a, u), axis=1)
    return (h.astype(x.dtype) * y_br) @ w_out + b_out


def stick_breaking_attention(q, k, v):
    B, S, H, Dh = q.shape
    scale = Dh ** -0.5
    outs = []
    for i in range(S // Q_BLOCK):
        q0 = i * Q_BLOCK
        kv_len = q0 + Q_BLOCK
        q_blk = q[:, q0:kv_len]
        k_p = k[:, :kv_len]
        v_p = v[:, :kv_len]
        z = jnp.einsum('bqhd,bkhd->bhqk', q_blk, k_p).astype(jnp.float32) * scale
        q_pos = q0 + jnp.arange(Q_BLOCK)
        k_pos = jnp.arange(kv_len)
        mask = k_pos[None, :] < q_pos[:, None]
        log_beta = jax.nn.log_sigmoid(z)
        log_1m = jnp.where(mask, jax.nn.log_sigmoid(-z), 0.0)
        suffix = lax.cumsum(log_1m, axis=3, reverse=True) - log_1m
        w = jnp.where(mask, jnp.exp(log_beta + suffix), 0.0)
        o = jnp.einsum('bhqk,bkhd->bqhd', w, v_p.astype(jnp.float32))
        outs.append(o.astype(q.dtype))
    return jnp.concatenate(outs, axis=1)


def conv_ffn(x, w_up, conv_w, conv_b, w_down):
    h = causal_depthwise_conv(x @ w_up, conv_w, conv_b)
    return (jax.nn.gelu(h[..., :D_FF]) * h[..., D_FF:]) @ w_down


def setup_inputs(seed: int = 0) -> dict:
    key = jax.random.key(seed)
    ks = jax.random.split(key, 24)
    f32 = jnp.float32

    def nrm(k, shape, fan_in, gain=1.0):
        return jax.random.normal(k, shape, f32) * (gain * fan_in ** -0.5)

    def small(k, shape):
        return 0.01 * jax.random.normal(k, shape, f32)

    W, HD = LRU_WIDTH, N_HEADS * HEAD_DIM
    u = jax.random.uniform(ks[8], (N_A, W), f32, minval=0.9, maxval=0.999)
    s = u ** (1.0 / LRU_C)
    lru_a_param = jnp.log(s) - jnp.log1p(-s)
    kv_w = jnp.concatenate([nrm(ks[11], (D_MODEL, HD), D_MODEL),
                            nrm(ks[12], (D_MODEL, HD), D_MODEL, DN_BETA)], axis=1)
    return {
        "x": jax.random.normal(ks[0], (BATCH, SEQ, D_MODEL), f32),
        "lru_w_in": nrm(ks[1], (N_A, D_MODEL, 2 * W), D_MODEL),
        "lru_b_in": small(ks[2], (N_A, 2 * W)),
        "lru_conv_w": nrm(ks[3], (N_A, LRU_CONV, W), LRU_CONV),
        "lru_conv_b": small(ks[4], (N_A, W)),
        "lru_w_gates": nrm(ks[5], (N_A, LRU_HEADS, LRU_BLOCK, 2 * LRU_BLOCK), LRU_BLOCK),
        "lru_b_gates": small(ks[6], (N_A, LRU_HEADS, 2 * LRU_BLOCK)),
        "lru_a_param": lru_a_param,
        "lru_w_out": nrm(ks[9], (N_A, W, D_MODEL), W, DN_BETA),
        "lru_b_out": small(ks[10], (N_A, D_MODEL)),
        "kv_w": kv_w,
        "attn_w_q": nrm(ks[13], (N_B, D_MODEL, HD), D_MODEL),
        "attn_w_out": nrm(ks[14], (N_B, HD, D_MODEL), HD, DN_BETA),
        "ffn_w_up": nrm(ks[15], (DEPTH, D_MODEL, 2 * D_FF), D_MODEL),
        "ffn_conv_w": nrm(ks[16], (DEPTH, FFN_CONV, 2 * D_FF), FFN_CONV),
        "ffn_conv_b": small(ks[17], (DEPTH, 2 * D_FF)),
        "ffn_w_down": nrm(ks[18], (DEPTH, D_FF, D_MODEL), D_FF, DN_BETA),
        "ln_g": 1.0 + small(ks[19], (DEPTH, 2, D_MODEL)),
        "ln_b": small(ks[20], (DEPTH, 2, D_MODEL)),
    }


def reference(x, lru_w_in, lru_b_in, lru_conv_w, lru_conv_b, lru_w_gates, lru_b_gates,
              lru_a_param, lru_w_out, lru_b_out, kv_w, attn_w_q, attn_w_out,
              ffn_w_up, ffn_conv_w, ffn_conv_b, ffn_w_down, ln_g, ln_b):
    B, S, _ = x.shape
    HD = N_HEADS * HEAD_DIM
    k_sh = None
    v_sh = None
    for layer in range(DEPTH):
        if layer < N_A:
            mix = rg_lru_block(x, lru_w_in[layer], lru_b_in[layer], lru_conv_w[layer],
                               lru_conv_b[layer], lru_w_gates[layer], lru_b_gates[layer],
                               lru_a_param[layer], lru_w_out[layer], lru_b_out[layer])
        else:
            if layer == N_A:
                kv = x @ kv_w
                k_sh = kv[..., :HD].reshape(B, S, N_HEADS, HEAD_DIM)
                v_sh = kv[..., HD:].reshape(B, S, N_HEADS, HEAD_DIM)
            j = layer - N_A
            q = (x @ attn_w_q[j]).reshape(B, S, N_HEADS, HEAD_DIM)
            o = stick_breaking_attention(q, k_sh, v_sh).reshape(B, S, HD)
            mix = o @ attn_w_out[j]
        x = layer_norm(DN_ALPHA * x + mix, ln_g[layer, 0], ln_b[layer, 0])
        f = conv_ffn(x, ffn_w_up[layer], ffn_conv_w[layer], ffn_conv_b[layer], ffn_w_down[layer])
        x = layer_norm(DN_ALPHA * x + f, ln_g[layer, 1], ln_b[layer, 1])
    return x
```

```python
import numpy as np
from contextlib import ExitStack, suppress
import concourse.bass as bass
import concourse.mybir as mybir
from concourse.bass_utils import run_bass_kernel_spmd

F32 = mybir.dt.float32
BF16 = mybir.dt.bfloat16
I32 = mybir.dt.int32
AF = mybir.ActivationFunctionType
ALU = mybir.AluOpType

D = 1024
S = 4096
TM = 1024
TT = 512
NMT = S // TM
NTT = TM // TT
DFF = 3072
ALPHA = 4.0 ** 0.25
EPS = 1e-5
SCALE = 128.0 ** -0.5
NCORES = 8
WORK_CORES = [0, 1, 4, 5]
STOP = None
STOP_MT = 0

PIECES = []
def _pc(name, n):
    PIECES.append((name, n))
_pc("win_g", 8192); _pc("win_r", 8192); _pc("wg", 2048); _pc("wout", 8192)
for g in range(6): _pc(f"ffn0_{g}", 12288)
_pc("wk", 8192); _pc("wv", 8192); _pc("wq", 8192); _pc("wo", 8192)
for g in range(6): _pc(f"ffn1_{g}", 12288)
POFF = {}
_o = 0
for n_, e_ in PIECES:
    POFF[n_] = (_o, e_); _o += e_
ETOT = _o
CH = 2048
assert ETOT % CH == 0

PRM = {}
_o = 0
def _pp(name, n):
    global _o
    PRM[name] = _o; _o += n
_pp("b_in", 16); _pp("lcw", 32); _pp("lcb", 8); _pp("bg", 16); _pp("apar", 8); _pp("b_out", 8)
_pp("fcw", 2 * 48 * 3); _pp("fcb", 2 * 48); _pp("lng", 32); _pp("lnb", 32)
NPRM = _o


class _StopBuild(Exception):
    pass


class Prog:
    def __init__(self, nc, es):
        self.nc = nc
        self.es = es
        self.eng = {"pe": nc.tensor, "act": nc.scalar, "dve": nc.vector, "pool": nc.gpsimd, "sp": nc.sync}
        self.sem = {e: es.enter_context(nc.semaphore("s_" + e)) for e in ("pe", "act", "dve", "pool")}
        self.cnt = {e: 0 for e in self.sem}
        self.dsem = {}
        self.dcnt = {}
        self.waited = {}
        self.last_w = {}
        self.readers = {}
        self.nops = {e: 0 for e in self.eng}

    def _wait(self, e, h):
        semkey, val, src = h
        if src == e and e == "pe":
            return
        k = (e, semkey)
        if self.waited.get(k, 0) >= val:
            return
        self.waited[k] = val
        sem = self.sem[semkey] if semkey in self.sem else self.dsem[semkey]
        self.eng[e].wait_ge(sem, val)

    def _deps(self, e, r, w):
        for k in r:
            h = self.last_w.get(k)
            if h is not None:
                self._wait(e, h)
        for k in w:
            h = self.last_w.get(k)
            if h is not None:
                self._wait(e, h)
            for h2 in self.readers.get(k, ()):
                self._wait(e, h2)

    def _commit(self, h, r, w):
        for k in w:
            self.last_w[k] = h
            self.readers[k] = []
        for k in r:
            self.readers.setdefault(k, []).append(h)

    def op(self, e, fn, r=(), w=(), inc=True):
        self._deps(e, r, w)
        ins = fn(self.eng[e])
        self.nops[e] += 1
        if inc:
            ins.then_inc(self.sem[e], 1)
            self.cnt[e] += 1
            h = (e, self.cnt[e], e)
        else:
            h = (e, self.cnt[e] + 1, e)
        self._commit(h, r, w)
        return h

    def dma(self, q, out, in_, r=(), w=(), stream="d"):
        if stream not in self.dsem:
            self.dsem[stream] = self.es.enter_context(self.nc.semaphore("d_" + stream))
            self.dcnt[stream] = 0
        self._deps(q, r, w)
        self.eng[q].dma_start(out=out, in_=in_).then_inc(self.dsem[stream], 16)
        self.nops[q] += 1
        self.dcnt[stream] += 16
        h = (stream, self.dcnt[stream], "dma")
        self._commit(h, r, w)
        return h

    def finish(self):
        for s, v in self.dcnt.items():
            self._wait("sp", (s, v, "dma"))
        for e, v in self.cnt.items():
            if v:
                self._wait("sp", (e, v, e))


def build():
    nc = bass.Bass("TRN2", target_bir_lowering=False)
    xT = nc.dram_tensor("xT", [128, 8, S], F32, kind="ExternalInput").ap()
    wall = nc.dram_tensor("wall", [128, ETOT], F32, kind="ExternalInput").ap()
    prm_d = nc.dram_tensor("prm", [128, NPRM], F32, kind="ExternalInput").ap()
    outT = nc.dram_tensor("outT", [128, 8, S], F32, kind="ExternalOutput").ap()
    wbf = nc.dram_tensor("wbf", [128, ETOT], BF16).ap()
    kT_d = nc.dram_tensor("kT_d", [8, 128, S], BF16).ap()
    v_d = nc.dram_tensor("v_d", [8, S, 128], BF16).ap()

    with suppress(_StopBuild), ExitStack() as es:
        P = Prog(nc, es)
        sb = lambda name, shape, dt: es.enter_context(nc.sbuf_tensor(name, shape, dt))
        xres = sb("xres", [128, 8, TM], F32)
        xbf = sb("xbf", [128, 8, TM], BF16)
        hy = sb("hy", [128, 8, TM], BF16)
        wsl = [sb(f"wsl{i}", [128, 12288], BF16) for i in range(2)]
        wgs = sb("wgs", [128, 8, 256], BF16)
        stg = [sb(f"stg{i}", [128, CH], BF16) for i in range(2)]
        kTh2 = [sb(f"kTh{i}", [128, S], BF16) for i in range(2)]
        vh2 = [sb(f"vh{i}", [128, S // 128, 128], BF16) for i in range(2)]
        qT2 = [sb(f"qT{i}", [128, TM], BF16) for i in range(2)]
        vst = [sb(f"vst{i}", [128, 1024], BF16) for i in range(2)]
        NTMP = 9
        TW = TT + 4
        Wf = [sb(f"Wf{i}", [128, 2 * TW], F32) for i in range(4)]
        tmp = []
        for i in range(4):
            tmp += [Wf[i][:, 0:TW], Wf[i][:, TW:2 * TW]]
        tmp.append(sb("tmp8", [128, TW], F32)[:])
        tmp9 = sb("tmp9", [128, TW], F32)[:]
        Bw = [sb(f"Bw{i}", [128, 2 * TT], BF16) for i in range(6)]
        tb16 = [Bw[i][:, 0:TT] for i in range(6)]
        RW = sb("RW", [128, 2 * TT], F32)
        msk = sb("msk", [128, 4, TT], BF16)
        nltri = sb("nltri", [128, 128], BF16)
        ones = sb("ones", [128, 128], BF16)
        prm = sb("prm_sb", [128, NPRM], F32)
        nc8 = sb("nc8", [128, 8], F32)
        hcar = sb("hcar", [128, 8], F32)
        xrcar = sb("xrcar", [128, 8, 3], F32)
        fcar = sb("fcar", [128, 2, 48, 2], F32)
        pw = [es.enter_context(nc.psum_tensor(f"pw{i}", [128, 2 * TT], F32)) for i in range(4)]
        ps = []
        for i in range(4):
            ps += [pw[i][:, 0:TT], pw[i][:, TT:2 * TT]]
        PSK = [("ps", i) for i in range(8)]
        print("sbuf bytes remaining:", nc.sbuf_bytes_remaining, flush=True)

        def stop_here(tag, mt=0):
            if STOP == tag and mt == STOP_MT:
                g0_ = mt * TM
                P.dma("pool", outT[:, :, g0_:g0_ + TM], xres[:], r=[("xres", m, t) for m in range(8) for t in range(NTT)],
                      w=[("out", mt)], stream="outst")
                P.finish()
                print("STOP at", tag, "ops:", P.nops, flush=True)
                raise _StopBuild()

        def gb(gsel):
            return Bw[2 + gsel // 2][:, (gsel % 2) * TT:(gsel % 2 + 1) * TT], f"tb{2 + gsel // 2}"

        def pcol(name, i):
            o = PRM[name] + i
            return prm[:, o:o + 1]

        P.dma("sp", prm[:], prm_d[:, :], w=["prm"], stream="misc")
        P.op("pool", lambda e: e.memset(hcar[:], 0.0), w=[("hcar", c_) for c_ in range(8)])
        P.op("pool", lambda e: e.memset(xrcar[:], 0.0), w=[("xrcar", c_) for c_ in range(8)])
        P.op("pool", lambda e: e.memset(fcar[:], 0.0), w=[("fcar", l_, q_) for l_ in range(2) for q_ in range(48)])
        P.op("pool", lambda e: e.memset(ones[:], 1.0), w=["ones"])
        io_i = tmp[7][:, 0:TT].bitcast(I32)
        P.op("pool", lambda e: e.iota(io_i, pattern=[[1, TT]], base=0, channel_multiplier=-1), w=["t7"])
        io_f = tmp[8][:, 0:TT]
        P.op("dve", lambda e: e.tensor_copy(out=io_f, in_=io_i), r=["t7"], w=["t8"])
        P.op("dve", lambda e: e.tensor_scalar(out=nltri[:], in0=io_f[:, 0:128], scalar1=0.0, scalar2=-1.0, op0=ALU.is_le, op1=ALU.mult),
             r=["t8"], w=["nltri"])
        for d in range(4):
            P.op("dve", lambda e, d=d: e.tensor_single_scalar(out=msk[:, d, :], in_=io_f, scalar=float(128 * d),
                                                          op=ALU.is_gt), r=["t8"], w=[("msk", d)])
        P.op("act", lambda e: e.activation(out=nc8[:], in_=prm[:, PRM["apar"]:PRM["apar"] + 8], func=AF.Exp, scale=-1.0),
             r=["prm"], w=["nc8"])
        P.op("act", lambda e: e.activation(out=nc8[:], in_=nc8[:], func=AF.Ln, bias=1.0), r=["nc8"], w=["nc8"])
        P.op("dve", lambda e: e.tensor_scalar(out=nc8[:], in0=nc8[:], scalar1=-8.0, scalar2=None, op0=ALU.mult),
             r=["nc8"], w=["nc8"])


        wchunks = {n_: [("wbf", c) for c in range(o_ // CH, (o_ + e_) // CH)] for n_, (o_, e_) in POFF.items()}
        cast_done = set()
        cast_ctr = [0]

        def cast_piece(name):
            if name in cast_done:
                return
            cast_done.add(name)
            o_, e_ = POFF[name]
            for ci in range(o_ // CH, (o_ + e_) // CH):
                s_ = cast_ctr[0] % 2
                cast_ctr[0] += 1
                P.dma("pool", stg[s_][:], wall[:, ci * CH:(ci + 1) * CH], w=[("stg", s_)], stream=f"cast{s_}")
                P.dma("pool", wbf[:, ci * CH:(ci + 1) * CH], stg[s_][:], r=[("stg", s_)], w=[("wbf", ci)], stream=f"wst{s_}")

        slot_ctr = [0]

        def load_w(name):
            cast_piece(name)
            o_, e_ = POFF[name]
            if name == "wg":
                P.dma("sp", wgs[:].rearrange("p a b -> p (a b)"), wbf[:, o_:o_ + e_], r=wchunks[name], w=["wgs"], stream="wg")
                return None
            s_ = slot_ctr[0] % 2
            slot_ctr[0] += 1
            P.dma("sp", wsl[s_][:, 0:e_], wbf[:, o_:o_ + e_], r=wchunks[name], w=[("wsl", s_)], stream=f"wl{s_}")
            return s_

        psrot = [0]

        def nps(lo=0, hi=8):
            i = lo + psrot[0] % (hi - lo)
            psrot[0] += 1
            return i

        def mm_group(pi, pairs, r, w_extra=()):
            n = len(pairs)
            h = None
            for k, (l, rr) in enumerate(pairs):
                h = P.op("pe", lambda e, l=l, rr=rr, k=k: e.matmul(ps[pi][:], lhsT=l, rhs=rr, start=(k == 0), stop=(k == n - 1)),
                         r=r, w=[PSK[pi]] + list(w_extra), inc=(k == n - 1))
            return h

        def layer_norm(t, lidx):
            tc_ = slice(t * TT, (t + 1) * TT)
            XR = [("xres", m, t) for m in range(8)]
            P.op("act", lambda e: e.activation(out=hy[:, :, tc_], in_=xres[:, :, tc_], func=AF.Square), r=XR, w=[("hy", t)])
            P.op("act", lambda e: e.activation(out=xbf[:, :, tc_], in_=xres[:, :, tc_], func=AF.Copy), r=XR, w=[("xbf", t)])
            p1, p2 = nps(), nps()
            mm_group(p1, [(ones[:], xbf[:, m, tc_]) for m in range(8)], r=["ones", ("xbf", t)])
            mm_group(p2, [(ones[:], hy[:, m, tc_]) for m in range(8)], r=["ones", ("hy", t)])
            mean, msq, rstd, tt_ = tmp[0], tmp[1], tmp[2], tmp[3]
            P.op("dve", lambda e: e.tensor_scalar(out=mean[:, 0:TT], in0=ps[p1][:], scalar1=1.0 / D, scalar2=None, op0=ALU.mult),
                 r=[PSK[p1]], w=["t0"])
            P.op("dve", lambda e: e.tensor_tensor(out=msq[:, 0:TT], in0=mean[:, 0:TT], in1=mean[:, 0:TT], op=ALU.mult),
                 r=["t0"], w=["t1"])
            P.op("dve", lambda e: e.scalar_tensor_tensor(out=msq[:, 0:TT], in0=ps[p2][:], scalar=1.0 / D, in1=msq[:, 0:TT],
                                                         op0=ALU.mult, op1=ALU.subtract), r=[PSK[p2], "t1"], w=["t1"])
            P.op("dve", lambda e: e.tensor_scalar(out=msq[:, 0:TT], in0=msq[:, 0:TT], scalar1=EPS, scalar2=None, op0=ALU.add),
                 r=["t1"], w=["t1"])
            P.op("act", lambda e: e.activation(out=rstd[:, 0:TT], in_=msq[:, 0:TT], func=AF.Sqrt), r=["t1"], w=["t2"])
            P.op("dve", lambda e: e.reciprocal(out=rstd[:, 0:TT], in_=rstd[:, 0:TT]), r=["t2"], w=["t2"])
            for m in range(8):
                tk = "t3" if m % 2 == 0 else "t4"
                tb_ = tmp[3] if m % 2 == 0 else tmp[4]
                P.op("dve", lambda e, m=m, tb_=tb_: e.tensor_tensor(out=tb_[:, 0:TT], in0=xres[:, m, tc_], in1=mean[:, 0:TT], op=ALU.subtract),
                     r=[("xres", m, t), "t0"], w=[tk])
                P.op("dve", lambda e, tb_=tb_: e.tensor_tensor(out=tb_[:, 0:TT], in0=tb_[:, 0:TT], in1=rstd[:, 0:TT], op=ALU.mult),
                     r=[tk, "t2"], w=[tk])
                P.op("act", lambda e, m=m, tb_=tb_: e.activation(out=xres[:, m, tc_], in_=tb_[:, 0:TT], func=AF.Identity,
                                                                scale=pcol("lng", lidx * 8 + m), bias=pcol("lnb", lidx * 8 + m)),
                     r=[tk, "prm"], w=[("xres", m, t)])
                P.op("act", lambda e, m=m, tb_=tb_: e.activation(out=xbf[:, m, tc_], in_=tb_[:, 0:TT], func=AF.Identity,
                                                                scale=pcol("lng", lidx * 8 + m), bias=pcol("lnb", lidx * 8 + m)),
                     r=[tk, "prm"], w=[("xbf", t)])

        def proj_res_ln(slot, src_key, bias_name, lidx):
            W = wsl[slot][:, 0:8192].rearrange("p (k n) -> p k n", k=8)
            for t in range(NTT):
                tc_ = slice(t * TT, (t + 1) * TT)
                for m in range(8):
                    pi = nps()
                    mm_group(pi, [(W[:, c, m * 128:(m + 1) * 128], hy[:, c, tc_]) for c in range(8)],
                             r=[("wsl", slot), (src_key, t)])
                    if bias_name is not None:
                        tk = "t5" if m % 2 == 0 else "t6"
                        tb_ = tmp[5] if m % 2 == 0 else tmp[6]
                        P.op("act", lambda e, m=m, tb_=tb_, pi=pi: e.activation(out=tb_[:, 0:TT], in_=ps[pi][:], func=AF.Identity,
                                                                               bias=pcol(bias_name, m)),
                             r=[PSK[pi], "prm"], w=[tk])
                        P.op("dve", lambda e, m=m, tb_=tb_: e.scalar_tensor_tensor(out=xres[:, m, tc_], in0=xres[:, m, tc_], scalar=ALPHA,
                                                                                 in1=tb_[:, 0:TT], op0=ALU.mult, op1=ALU.add),
                             r=[tk, ("xres", m, t)], w=[("xres", m, t)])
                    else:
                        P.op("dve", lambda e, m=m, pi=pi: e.scalar_tensor_tensor(out=xres[:, m, tc_], in0=xres[:, m, tc_], scalar=ALPHA,
                                                                               in1=ps[pi][:], op0=ALU.mult, op1=ALU.add),
                             r=[PSK[pi], ("xres", m, t)], w=[("xres", m, t)])
            for t in range(NTT):
                layer_norm(t, lidx)

        def ffn(layer, lidx):
            units = [(g, t) for g in range(6) for t in range(NTT)]
            slots = {}

            def up(g, t, js=range(4)):
                if g not in slots:
                    slots[g] = load_w(f"ffn{layer}_{g}")
                slot = slots[g]
                WU = wsl[slot][:, 0:8192].rearrange("p (k n) -> p k n", k=8)
                tc_ = slice(t * TT, (t + 1) * TT)
                for j in js:
                    res = []
                    for br in range(2):
                        q = br * 24 + g * 4 + j
                        pi = nps(0, 4)
                        mm_group(pi, [(WU[:, kc, br * 512 + j * 128: br * 512 + (j + 1) * 128], xbf[:, kc, tc_]) for kc in range(8)],
                                 r=[("wsl", slot), ("xbf", t)])
                        bi = 4 * (j % 2) + 2 * br
                        hb, cv = tmp[bi], tmp[bi + 1]
                        hk, ck = f"t{bi}", f"t{bi + 1}"
                        wo_ = PRM["fcw"] + (layer * 48 + q) * 3
                        bo_ = PRM["fcb"] + layer * 48 + q
                        P.op("act", lambda e, hb=hb, q=q: e.activation(out=hb[:, 0:2], in_=fcar[:, layer, q, :], func=AF.Copy), r=[("fcar", layer, q)], w=[hk])
                        P.op("act", lambda e, hb=hb, pi=pi: e.activation(out=hb[:, 2:TT + 2], in_=ps[pi][:], func=AF.Copy),
                             r=[PSK[pi]], w=[hk])
                        P.op("act", lambda e, cv=cv, pi=pi, wo_=wo_, bo_=bo_: e.activation(
                            out=cv[:, 0:TT], in_=ps[pi][:], func=AF.Identity, scale=prm[:, wo_ + 2:wo_ + 3], bias=prm[:, bo_:bo_ + 1]),
                            r=[PSK[pi], "prm"], w=[ck])
                        P.op("act", lambda e, hb=hb, q=q: e.activation(out=fcar[:, layer, q, :], in_=hb[:, TT:TT + 2], func=AF.Copy), r=[hk], w=[("fcar", layer, q)])
                        P.op("dve", lambda e, hb=hb, cv=cv, wo_=wo_: e.scalar_tensor_tensor(
                            out=cv[:, 0:TT], in0=hb[:, 1:TT + 1], scalar=prm[:, wo_ + 1:wo_ + 2], in1=cv[:, 0:TT], op0=ALU.mult, op1=ALU.add),
                            r=[hk, ck, "prm"], w=[ck])
                        P.op("dve", lambda e, hb=hb, cv=cv, wo_=wo_: e.scalar_tensor_tensor(
                            out=cv[:, 0:TT], in0=hb[:, 0:TT], scalar=prm[:, wo_:wo_ + 1], in1=cv[:, 0:TT], op0=ALU.mult, op1=ALU.add),
                            r=[hk, ck, "prm"], w=[ck])
                        res.append((cv, ck))
                    (ca, cak), (cb, cbk) = res
                    P.op("act", lambda e, ca=ca: e.activation(out=ca[:, 0:TT], in_=ca[:, 0:TT], func=AF.Gelu_apprx_tanh), r=[cak], w=[cak])
                    gsel = (t % 2) * 4 + j
                    gap, gkey = gb(gsel)
                    P.op("dve", lambda e, ca=ca, cb=cb, gap=gap: e.tensor_tensor(out=gap, in0=ca[:, 0:TT], in1=cb[:, 0:TT], op=ALU.mult),
                         r=[cak, cbk], w=[gkey])

            def down(g, t, ms=range(8)):
                slot = slots[g]
                WD = wsl[slot][:, 8192:12288].rearrange("p (k n) -> p k n", k=4)
                tc_ = slice(t * TT, (t + 1) * TT)
                for m in ms:
                    pi = nps(4, 8)
                    mm_group(pi, [(WD[:, j, m * 128:(m + 1) * 128], gb((t % 2) * 4 + j)[0]) for j in range(4)],
                             r=[("wsl", slot)] + [gb((t % 2) * 4 + j)[1] for j in range(4)])
                    if g == 0:
                        P.op("dve", lambda e, m=m, pi=pi: e.scalar_tensor_tensor(out=xres[:, m, tc_], in0=xres[:, m, tc_], scalar=ALPHA,
                                                                               in1=ps[pi][:], op0=ALU.mult, op1=ALU.add),
                             r=[PSK[pi], ("xres", m, t)], w=[("xres", m, t)])
                    else:
                        P.op("dve", lambda e, m=m, pi=pi: e.tensor_tensor(out=xres[:, m, tc_], in0=xres[:, m, tc_], in1=ps[pi][:], op=ALU.add),
                             r=[PSK[pi], ("xres", m, t)], w=[("xres", m, t)])

            for ui, (g, t) in enumerate(units):
                for j in range(4):
                    up(g, t, [j])
                    if ui > 0:
                        down(*units[ui - 1], ms=[2 * j, 2 * j + 1])
            down(*units[-1])
            for t in range(NTT):
                layer_norm(t, lidx)

        for mt in range(NMT):
            g0 = mt * TM
            XRALL = [("xres", m, t) for m in range(8) for t in range(NTT)]
            P.dma("pool", xres[:], xT[:, :, g0:g0 + TM], w=XRALL, stream="xin")
            if mt == 0:
                for n_, _e in PIECES:
                    cast_piece(n_)
            for t in range(NTT):
                tc_ = slice(t * TT, (t + 1) * TT)
                P.op("act", lambda e, tc_=tc_: e.activation(out=xbf[:, :, tc_], in_=xres[:, :, tc_], func=AF.Copy),
                     r=[("xres", m, t) for m in range(8)], w=[("xbf", t)])

            stop_here("xload", mt)
            sg = load_w("win_g")
            sr = load_w("win_r")
            load_w("wg")
            stop_here("cast", mt)
            WG = wsl[sg][:, 0:8192].rearrange("p (k n) -> p k n", k=8)
            WR = wsl[sr][:, 0:8192].rearrange("p (k n) -> p k n", k=8)
            gi, gr, a_, m2, u_, h_ = tmp[2:8]

            def geluG(t):
                tc_ = slice(t * TT, (t + 1) * TT)
                for c in range(8):
                    py = nps()
                    mm_group(py, [(WG[:, kc, c * 128:(c + 1) * 128], xbf[:, kc, tc_]) for kc in range(8)], r=[("wsl", sg), ("xbf", t)])
                    P.op("act", lambda e, c=c, py=py: e.activation(out=hy[:, c, tc_], in_=ps[py][:], func=AF.Gelu_apprx_tanh, bias=pcol("b_in", c)),
                         r=[PSK[py], "prm"], w=[("hy", t)])

            def bufsA(n):
                s_ = n % 2
                return ((tmp[0], "t0") if s_ == 0 else (tmp[8], "t8"), (tmp[1], "t1") if s_ == 0 else (tmp9, "t9"), (tb16[s_], f"tb{s_}"))

            def stageA(n):
                t, c = divmod(n, 8)
                tc_ = slice(t * TT, (t + 1) * TT)
                (xr, xrk), (xbr, xbrk), (xbrb, xbk) = bufsA(n)
                pr = nps()
                mm_group(pr, [(WR[:, kc, c * 128:(c + 1) * 128], xbf[:, kc, tc_]) for kc in range(8)], r=[("wsl", sr), ("xbf", t)])
                P.op("dve", lambda e: e.tensor_copy(out=xr[:, 0:3], in_=xrcar[:, c, :]), r=[("xrcar", c)], w=[xrk])
                P.op("act", lambda e: e.activation(out=xr[:, 3:TT + 3], in_=ps[pr][:], func=AF.Identity, bias=pcol("b_in", 8 + c)),
                     r=[PSK[pr], "prm"], w=[xrk])
                P.op("dve", lambda e: e.tensor_copy(out=xrcar[:, c, :], in_=xr[:, TT:TT + 3]), r=[xrk], w=[("xrcar", c)])
                lw = PRM["lcw"] + c * 4
                P.op("dve", lambda e: e.tensor_scalar(out=xbr[:, 0:TT], in0=xr[:, 3:TT + 3], scalar1=prm[:, lw + 3:lw + 4],
                                                      scalar2=pcol("lcb", c), op0=ALU.mult, op1=ALU.add), r=[xrk, "prm"], w=[xbrk])
                for k in (2, 1, 0):
                    P.op("dve", lambda e, k=k: e.scalar_tensor_tensor(out=xbr[:, 0:TT], in0=xr[:, k:TT + k], scalar=prm[:, lw + k:lw + k + 1],
                                                                     in1=xbr[:, 0:TT], op0=ALU.mult, op1=ALU.add), r=[xrk, xbrk, "prm"], w=[xbrk])
                P.op("act", lambda e: e.activation(out=xbrb, in_=xbr[:, 0:TT], func=AF.Copy), r=[xbrk], w=[xbk])

            setB = [
                dict(gi=(tmp[2], "t2"), gr=(tmp[3], "t3"), a=(tmp[4], "t4"), m2=(tmp[5], "t5"), u=(tmp[6], "t6"), h=(tmp[7], "t7")),
                dict(gi=(Bw[2][:].bitcast(F32), "tb2"), gr=(Bw[3][:].bitcast(F32), "tb3"), a=(Bw[4][:].bitcast(F32), "tb4"),
                     m2=(Bw[5][:].bitcast(F32), "tb5"), u=(RW[:, 0:TT], "RW"), h=(RW[:, TT:2 * TT], "RW")),
            ]

            def stageB(n):
                t, c = divmod(n, 8)
                tc_ = slice(t * TT, (t + 1) * TT)
                (xr, xrk), (xbr, xbrk), (xbrb, xbk) = bufsA(n)
                B_ = setB[n % 2]
                (gi, gik), (gr, grk), (a_, ak), (m2, m2k), (u_, uk), (h_, hk_) = B_["gi"], B_["gr"], B_["a"], B_["m2"], B_["u"], B_["h"]
                pgi, pgr = nps(), nps()
                mm_group(pgi, [(wgs[:, c, 0:128], xbrb)], r=["wgs", xbk])
                mm_group(pgr, [(wgs[:, c, 128:256], xbrb)], r=["wgs", xbk])
                yield
                P.op("act", lambda e: e.activation(out=gi[:, 0:TT], in_=ps[pgi][:], func=AF.Sigmoid, bias=pcol("bg", c)),
                     r=[PSK[pgi], "prm"], w=[gik])
                P.op("act", lambda e: e.activation(out=gr[:, 0:TT], in_=ps[pgr][:], func=AF.Sigmoid, bias=pcol("bg", 8 + c)),
                     r=[PSK[pgr], "prm"], w=[grk])
                yield
                P.op("act", lambda e: e.activation(out=a_[:, 0:TT], in_=gr[:, 0:TT], func=AF.Exp, scale=nc8[:, c:c + 1]),
                     r=[grk, "nc8"], w=[ak])
                yield
                P.op("dve", lambda e: e.scalar_tensor_tensor(out=m2[:, 0:TT], in0=a_[:, 0:TT], scalar=-1.0, in1=a_[:, 0:TT], op0=ALU.mult, op1=ALU.mult),
                     r=[ak], w=[m2k])
                P.op("dve", lambda e: e.tensor_scalar(out=m2[:, 0:TT], in0=m2[:, 0:TT], scalar1=1.0, scalar2=1e-30, op0=ALU.add, op1=ALU.max),
                     r=[m2k], w=[m2k])
                yield
                P.op("act", lambda e: e.activation(out=m2[:, 0:TT], in_=m2[:, 0:TT], func=AF.Sqrt), r=[m2k], w=[m2k])
                if mt == 0 and t == 0:
                    P.op("dve", lambda e: e.memset(m2[:, 0:1], 1.0), r=[m2k], w=[m2k])
                P.op("dve", lambda e: e.tensor_tensor(out=u_[:, 0:TT], in0=gi[:, 0:TT], in1=xbr[:, 0:TT], op=ALU.mult), r=[gik, xbrk], w=[uk])
                yield
                P.op("dve", lambda e: e.tensor_tensor(out=u_[:, 0:TT], in0=u_[:, 0:TT], in1=m2[:, 0:TT], op=ALU.mult), r=[uk, m2k], w=[uk])
                P.op("dve", lambda e: e.tensor_tensor_scan(out=h_[:, 0:TT], data0=a_[:, 0:TT], data1=u_[:, 0:TT], initial=hcar[:, c:c + 1],
                                                           op0=ALU.mult, op1=ALU.add), r=[ak, uk, ("hcar", c)], w=[hk_])
                yield
                P.op("dve", lambda e: e.tensor_copy(out=hcar[:, c:c + 1], in_=h_[:, TT - 1:TT]), r=[hk_], w=[("hcar", c)])
                P.op("dve", lambda e: e.tensor_tensor(out=hy[:, c, tc_], in0=h_[:, 0:TT], in1=hy[:, c, tc_], op=ALU.mult),
                     r=[hk_, ("hy", t)], w=[("hy", t)])
                yield

            nA = NTT * 8
            geluG(0)
            for n in range(0, nA, 2):
                if n % 8 == 0 and n > 0:
                    geluG(n // 8)
                stageA(n)
                stageA(n + 1)
                g0_, g1_ = stageB(n), stageB(n + 1)
                d0 = d1 = False
                while not (d0 and d1):
                    if not d0:
                        d0 = next(g0_, "end") == "end"
                    if not d1:
                        d1 = next(g1_, "end") == "end"
            stop_here("mixer", mt)
            so = load_w("wout")
            proj_res_ln(so, "hy", "b_out", 0)
            stop_here("ln0", mt)
            ffn(0, 1)
            stop_here("ffn0", mt)

            sk = load_w("wk")
            WK = wsl[sk][:, 0:8192].rearrange("p (k n) -> p k n", k=8)
            for t in range(NTT):
                tc_ = slice(t * TT, (t + 1) * TT)
                for h in range(8):
                    pi = nps()
                    mm_group(pi, [(WK[:, kc, h * 128:(h + 1) * 128], xbf[:, kc, tc_]) for kc in range(8)], r=[("wsl", sk), ("xbf", t)])
                    P.op("act", lambda e, h=h, pi=pi, tc_=tc_: e.activation(out=hy[:, h, tc_], in_=ps[pi][:], func=AF.Copy),
                         r=[PSK[pi]], w=[("hy", t)])
            P.dma("pool", kT_d.rearrange("h d s -> d h s")[:, :, g0:g0 + TM], hy[:], r=[("hy", t) for t in range(NTT)], w=[("kd", mt)], stream="kst")
            sv = load_w("wv")
            WV = wsl[sv][:, 0:8192].rearrange("p (k n) -> p k n", k=8)
            for tb in range(TM // 128):
                t = tb // 4
                vs = vst[tb % 2]
                for half in range(2):
                    pi = nps()
                    mm_group(pi, [(xbf[:, kc, tb * 128:(tb + 1) * 128], WV[:, kc, half * 512:(half + 1) * 512]) for kc in range(8)],
                             r=[("wsl", sv), ("xbf", t)])
                    P.op("act", lambda e, vs=vs, half=half, pi=pi: e.activation(out=vs[:, half * 512:(half + 1) * 512], in_=ps[pi][:], func=AF.Copy),
                         r=[PSK[pi]], w=[("vst", tb % 2)])
                P.dma("pool", v_d.rearrange("h s d -> s h d")[g0 + tb * 128: g0 + (tb + 1) * 128], vs[:].rearrange("p (h d) -> p h d", h=8),
                      r=[("vst", tb % 2)], w=[("vd", mt, tb)], stream=f"vst{tb % 2}")

            stop_here("kv", mt)
            sq = load_w("wq")
            WQ = wsl[sq][:, 0:8192].rearrange("p (k n) -> p k n", k=8)
            nk = g0 + TM
            qbA = g0 // 128
            its = []
            for h in range(8):
                for li, kb in enumerate(range(qbA + 7, -1, -1)):
                    half = kb >= qbA + 4
                    lo = TT if half else 0
                    masks = []
                    if half:
                        masks.append((TT, kb - qbA - 4))
                    elif kb >= qbA:
                        masks.append((0, kb - qbA))
                    its.append(dict(h=h, kb=kb, lo=lo, masks=masks, first1=(kb == qbA + 7), first0=(kb == qbA + 3), last=(kb == 0), li=li))
            setup_done = set()
            ZK = [[PSK[0], PSK[1]], [PSK[2], PSK[3]]]
            CK = [PSK[4], PSK[5]]

            def head_setup(h):
                if h in setup_done or h >= 8:
                    return
                setup_done.add(h)
                hb_ = h % 2
                for t in range(NTT):
                    tc_ = slice(t * TT, (t + 1) * TT)
                    pi = 4 + t
                    mm_group(pi, [(WQ[:, kc, h * 128:(h + 1) * 128], xbf[:, kc, tc_]) for kc in range(8)], r=[("wsl", sq), ("xbf", t)])
                    P.op("act", lambda e, pi=pi, tc_=tc_, hb_=hb_: e.activation(out=qT2[hb_][:, tc_], in_=ps[pi][:], func=AF.Copy, scale=SCALE),
                         r=[PSK[pi]], w=[("qT", hb_, t)])
                P.dma("pool", kTh2[hb_][:, 0:nk], kT_d[h, :, 0:nk], r=[("kd", m_) for m_ in range(mt + 1)], w=[("kTh", hb_)], stream=f"kld{hb_}")
                P.dma("pool", vh2[hb_][:, 0:nk // 128, :], v_d[h, 0:nk, :].rearrange("(kb p) d -> p kb d", p=128),
                      r=[("vd", m_, tb) for m_ in range(mt + 1) for tb in range(8)], w=[("vh", hb_)], stream=f"vld{hb_}")

            def halves(I):
                return [1] if I["lo"] else [0, 1]

            def S1(i, I):
                h, kb, lo = I["h"], I["kb"], I["lo"]
                hb_ = h % 2
                zb = i % 2
                eW, ekeys = Wf[zb], [f"t{2 * zb}", f"t{2 * zb + 1}"]
                for q in halves(I):
                    mm_group(2 * zb + q, [(kTh2[hb_][:, kb * 128:(kb + 1) * 128], qT2[hb_][:, q * TT:(q + 1) * TT])],
                             r=[("kTh", hb_), ("qT", hb_, q)])
                P.op("act", lambda e: e.activation(out=eW[:, lo:2 * TT], in_=pw[zb][:, lo:2 * TT], func=AF.Exp),
                     r=[ZK[zb][q] for q in halves(I)], w=ekeys)

            def S1b(i, I):
                lo = I["lo"]
                zb = i % 2
                eW, ekeys = Wf[zb], [f"t{2 * zb}", f"t{2 * zb + 1}"]
                spW, spk = Bw[i % 3], f"tb{i % 3}"
                P.op("act", lambda e: e.activation(out=spW[:, lo:2 * TT], in_=eW[:, lo:2 * TT], func=AF.Ln, bias=1.0), r=ekeys, w=[spk])
                for (ml, dd) in I["masks"]:
                    P.op("pool", lambda e, ml=ml, dd=dd: e.tensor_tensor(out=spW[:, ml:ml + TT], in0=spW[:, ml:ml + TT], in1=msk[:, dd, :], op=ALU.mult),
                         r=[spk, ("msk", dd)], w=[spk])

            def S2a(i, I):
                kb, lo = I["kb"], I["lo"]
                zb = i % 2
                spW, spk = Bw[i % 3], f"tb{i % 3}"
                tWb, tkeys = Wf[2 + zb], [f"t{4 + 2 * zb}", f"t{5 + 2 * zb}"]
                if I["first1"]:
                    P.op("dve", lambda e: e.memset(RW[:], 0.0), w=["RW"])
                for q in halves(I):
                    P.op("pe", lambda e, q=q: e.matmul(ps[2 * zb + q][:], lhsT=nltri[:], rhs=spW[:, q * TT:(q + 1) * TT], start=False, stop=True,
                                                       skip_group_check=True), r=["nltri", spk], w=[ZK[zb][q]], inc=True)
                if kb > 0:
                    for q in halves(I):
                        mm_group(4 + q, [(ones[:], spW[:, q * TT:(q + 1) * TT])], r=["ones", spk])
                P.op("dve", lambda e: e.tensor_tensor(out=tWb[:, lo:2 * TT], in0=pw[zb][:, lo:2 * TT], in1=RW[:, lo:2 * TT], op=ALU.subtract),
                     r=[ZK[zb][q] for q in halves(I)] + ["RW"], w=tkeys)
                if kb > 0:
                    P.op("dve", lambda e: e.tensor_tensor(out=RW[:, lo:2 * TT], in0=RW[:, lo:2 * TT], in1=pw[2][:, lo:2 * TT], op=ALU.add),
                         r=[CK[q] for q in halves(I)] + ["RW"], w=["RW"])

            def S2b(i, I):
                lo = I["lo"]
                zb = i % 2
                tWb, tkeys = Wf[2 + zb], [f"t{4 + 2 * zb}", f"t{5 + 2 * zb}"]
                wW, wk_ = Bw[3 + i % 3], f"tb{3 + i % 3}"
                P.op("act", lambda e: e.activation(out=wW[:, lo:2 * TT], in_=tWb[:, lo:2 * TT], func=AF.Exp), r=tkeys, w=[wk_])
                for (ml, dd) in I["masks"]:
                    P.op("pool", lambda e, ml=ml, dd=dd: e.tensor_tensor(out=wW[:, ml:ml + TT], in0=wW[:, ml:ml + TT], in1=msk[:, dd, :], op=ALU.mult),
                         r=[wk_, ("msk", dd)], w=[wk_])

            def S3(i, I):
                h, kb = I["h"], I["kb"]
                hb_ = h % 2
                wW, wk_ = Bw[3 + i % 3], f"tb{3 + i % 3}"
                for q in halves(I):
                    first = I["first1"] if q == 1 else I["first0"]
                    P.op("pe", lambda e, q=q, first=first: e.matmul(ps[6 + q][:], lhsT=vh2[hb_][:, kb, :], rhs=wW[:, q * TT:(q + 1) * TT],
                                                                   start=first, stop=I["last"]),
                         r=[("vh", hb_), wk_], w=[PSK[6 + q]], inc=True)
                if I["last"]:
                    for q in range(2):
                        P.op("act", lambda e, q=q: e.activation(out=hy[:, h, q * TT:(q + 1) * TT], in_=ps[6 + q][:], func=AF.Copy),
                             r=[PSK[6 + q]], w=[("hy", q)])

            n_it = len(its)
            head_setup(0)
            for k in range(n_it + 3):
                if k < n_it:
                    I = its[k]
                    if I["li"] == 3:
                        head_setup(I["h"] + 1)
                    S1(k, I)
                if 0 <= k - 1 < n_it:
                    S2a(k - 1, its[k - 1])
                if 0 <= k - 2 < n_it:
                    S2b(k - 2, its[k - 2])
                if k < n_it:
                    S1b(k, its[k])
                if 0 <= k - 3 < n_it:
                    S3(k - 3, its[k - 3])
            stop_here("attn", mt)
            sw = load_w("wo")
            proj_res_ln(sw, "hy", None, 2)
            stop_here("ln2", mt)
            ffn(1, 3)
            P.dma("pool", outT[:, :, g0:g0 + TM], xres[:], r=XRALL, w=[("out", mt)], stream="outst")
        P.finish()
        print("ops per engine:", P.nops, "sem counts:", P.cnt, flush=True)
    return nc


def _host_layout(inputs):
    f = lambda a: np.asarray(a, dtype=np.float32)
    def kmaj(w):
        K = w.shape[0] // 128
        return np.ascontiguousarray(w.reshape(K, 128, w.shape[1]).transpose(1, 0, 2)).reshape(128, -1)
    pieces = {}
    w_in = f(inputs["lru_w_in"])[0]
    pieces["win_g"] = kmaj(w_in[:, :1024]); pieces["win_r"] = kmaj(w_in[:, 1024:])
    pieces["wg"] = np.ascontiguousarray(f(inputs["lru_w_gates"])[0].transpose(1, 0, 2)).reshape(128, -1)
    pieces["wout"] = kmaj(f(inputs["lru_w_out"])[0])
    kv = f(inputs["kv_w"])
    pieces["wk"] = kmaj(kv[:, :1024]); pieces["wv"] = kmaj(kv[:, 1024:])
    pieces["wq"] = kmaj(f(inputs["attn_w_q"])[0]); pieces["wo"] = kmaj(f(inputs["attn_w_out"])[0])
    for l in range(2):
        wu = f(inputs["ffn_w_up"])[l]; wd = f(inputs["ffn_w_down"])[l]
        for g in range(6):
            up = np.concatenate([wu[:, g * 512:(g + 1) * 512], wu[:, DFF + g * 512: DFF + (g + 1) * 512]], axis=1)
            pieces[f"ffn{l}_{g}"] = np.concatenate([kmaj(up), kmaj(wd[g * 512:(g + 1) * 512])], axis=1)
    wall = np.concatenate([pieces[n] for n, _ in PIECES], axis=1)
    assert wall.shape == (128, ETOT)
    prm = np.zeros((128, NPRM), np.float32)
    cm = lambda v: np.ascontiguousarray(v.reshape(-1, 128).T)
    prm[:, PRM["b_in"]:PRM["b_in"] + 16] = cm(f(inputs["lru_b_in"])[0])
    lcw = f(inputs["lru_conv_w"])[0]
    prm[:, PRM["lcw"]:PRM["lcw"] + 32] = lcw.reshape(4, 8, 128).transpose(2, 1, 0).reshape(128, 32)
    prm[:, PRM["lcb"]:PRM["lcb"] + 8] = cm(f(inputs["lru_conv_b"])[0])
    bg = f(inputs["lru_b_gates"])[0]
    prm[:, PRM["bg"]:PRM["bg"] + 8] = bg[:, :128].T
    prm[:, PRM["bg"] + 8:PRM["bg"] + 16] = bg[:, 128:].T
    prm[:, PRM["apar"]:PRM["apar"] + 8] = cm(f(inputs["lru_a_param"])[0])
    prm[:, PRM["b_out"]:PRM["b_out"] + 8] = cm(f(inputs["lru_b_out"])[0])
    fcw = f(inputs["ffn_conv_w"])
    prm[:, PRM["fcw"]:PRM["fcw"] + 288] = fcw.reshape(2, 3, 48, 128).transpose(3, 0, 2, 1).reshape(128, 288)
    fcb = f(inputs["ffn_conv_b"])
    prm[:, PRM["fcb"]:PRM["fcb"] + 96] = fcb.reshape(2, 48, 128).transpose(2, 0, 1).reshape(128, 96)
    prm[:, PRM["lng"]:PRM["lng"] + 32] = f(inputs["ln_g"]).reshape(4, 8, 128).transpose(2, 0, 1).reshape(128, 32)
    prm[:, PRM["lnb"]:PRM["lnb"] + 32] = f(inputs["ln_b"]).reshape(4, 8, 128).transpose(2, 0, 1).reshape(128, 32)
    x = f(inputs["x"])
    xTs = [np.ascontiguousarray(x[b].T.reshape(8, 128, S).transpose(1, 0, 2)) for b in range(4)]
    return wall, prm, xTs


_NC_CACHE = []


def kernel(**inputs):
    wall, prm, xTs = _host_layout(inputs)
    if not _NC_CACHE:
        _NC_CACHE.append(build())
    nc = _NC_CACHE[0]
    zx, zw, zp = np.zeros_like(xTs[0]), np.zeros_like(wall), np.zeros_like(prm)
    in_maps = []
    for c in range(NCORES):
        if c in WORK_CORES:
            in_maps.append({"xT": xTs[WORK_CORES.index(c)], "wall": wall, "prm": prm})
        else:
            in_maps.append({"xT": zx, "wall": zw, "prm": zp})
    res = run_bass_kernel_spmd(nc, in_maps, core_ids=list(range(NCORES)))
    out = np.empty((4, S, D), np.float32)
    for b in range(4):
        o = np.asarray(res.results[WORK_CORES[b]]["outT"], dtype=np.float32)
        out[b] = o.transpose(2, 1, 0).reshape(S, D)
    return out
```

```python
import numpy as np
from contextlib import ExitStack, suppress
import concourse.bass as bass
import concourse.mybir as mybir
from concourse.bass_utils import run_bass_kernel_spmd

F32 = mybir.dt.float32
BF16 = mybir.dt.bfloat16
I32 = mybir.dt.int32
AF = mybir.ActivationFunctionType
ALU = mybir.AluOpType

D = 1024
S = 4096
TM = 1024
TT = 512
NMT = S // TM
NTT = TM // TT
DFF = 3072
ALPHA = 4.0 ** 0.25
EPS = 1e-5
SCALE = 128.0 ** -0.5
NCORES = 8
WORK_CORES = [0, 1, 4, 5]
STOP = None
STOP_MT = 0

PIECES = []
def _pc(name, n):
    PIECES.append((name, n))
_pc("win_g", 8192); _pc("win_r", 8192); _pc("wg", 2048); _pc("wout", 8192)
for g in range(6): _pc(f"ffn0_{g}", 12288)
_pc("wk", 8192); _pc("wv", 8192); _pc("wq", 8192); _pc("wo", 8192)
for g in range(6): _pc(f"ffn1_{g}", 12288)
POFF = {}
_o = 0
for n_, e_ in PIECES:
    POFF[n_] = (_o, e_); _o += e_
ETOT = _o
CH = 2048
assert ETOT % CH == 0

PRM = {}
_o = 0
def _pp(name, n):
    global _o
    PRM[name] = _o; _o += n
_pp("b_in", 16); _pp("lcw", 32); _pp("lcb", 8); _pp("bg", 16); _pp("apar", 8); _pp("b_out", 8)
_pp("fcw", 2 * 48 * 3); _pp("fcb", 2 * 48); _pp("lng", 32); _pp("lnb", 32)
NPRM = _o


class _StopBuild(Exception):
    pass


class Prog:
    def __init__(self, nc, es):
        self.nc = nc
        self.es = es
        self.eng = {"pe": nc.tensor, "act": nc.scalar, "dve": nc.vector, "pool": nc.gpsimd, "sp": nc.sync}
        self.sem = {e: es.enter_context(nc.semaphore("s_" + e)) for e in ("pe", "act", "dve", "pool")}
        self.cnt = {e: 0 for e in self.sem}
        self.dsem = {}
        self.dcnt = {}
        self.waited = {}
        self.last_w = {}
        self.readers = {}
        self.nops = {e: 0 for e in self.eng}

    def _wait(self, e, h):
        semkey, val, src = h
        if src == e and e == "pe":
            return
        k = (e, semkey)
        if self.waited.get(k, 0) >= val:
            return
        self.waited[k] = val
        sem = self.sem[semkey] if semkey in self.sem else self.dsem[semkey]
        self.eng[e].wait_ge(sem, val)

    def _deps(self, e, r, w):
        for k in r:
            h = self.last_w.get(k)
            if h is not None:
                self._wait(e, h)
        for k in w:
            h = self.last_w.get(k)
            if h is not None:
                self._wait(e, h)
            for h2 in self.readers.get(k, ()):
                self._wait(e, h2)

    def _commit(self, h, r, w):
        for k in w:
            self.last_w[k] = h
            self.readers[k] = []
        for k in r:
            self.readers.setdefault(k, []).append(h)

    def op(self, e, fn, r=(), w=(), inc=True):
        self._deps(e, r, w)
        ins = fn(self.eng[e])
        self.nops[e] += 1
        if inc:
            ins.then_inc(self.sem[e], 1)
            self.cnt[e] += 1
            h = (e, self.cnt[e], e)
        else:
            h = (e, self.cnt[e] + 1, e)
        self._commit(h, r, w)
        return h

    def dma(self, q, out, in_, r=(), w=(), stream="d"):
        if stream not in self.dsem:
            self.dsem[stream] = self.es.enter_context(self.nc.semaphore("d_" + stream))
            self.dcnt[stream] = 0
        self._deps(q, r, w)
        self.eng[q].dma_start(out=out, in_=in_).then_inc(self.dsem[stream], 16)
        self.nops[q] += 1
        self.dcnt[stream] += 16
        h = (stream, self.dcnt[stream], "dma")
        self._commit(h, r, w)
        return h

    def finish(self):
        for s, v in self.dcnt.items():
            self._wait("sp", (s, v, "dma"))
        for e, v in self.cnt.items():
            if v:
                self._wait("sp", (e, v, e))


def build():
    nc = bass.Bass("TRN2", target_bir_lowering=False)
    xT = nc.dram_tensor("xT", [128, 8, S], F32, kind="ExternalInput").ap()
    wall = nc.dram_tensor("wall", [128, ETOT], F32, kind="ExternalInput").ap()
    prm_d = nc.dram_tensor("prm", [128, NPRM], F32, kind="ExternalInput").ap()
    outT = nc.dram_tensor("outT", [128, 8, S], F32, kind="ExternalOutput").ap()
    wbf = nc.dram_tensor("wbf", [128, ETOT], BF16).ap()
    kT_d = nc.dram_tensor("kT_d", [8, 128, S], BF16).ap()
    v_d = nc.dram_tensor("v_d", [8, S, 128], BF16).ap()

    with suppress(_StopBuild), ExitStack() as es:
        P = Prog(nc, es)
        sb = lambda name, shape, dt: es.enter_context(nc.sbuf_tensor(name, shape, dt))
        xres = sb("xres", [128, 8, TM], F32)
        xbf = sb("xbf", [128, 8, TM], BF16)
        hy = sb("hy", [128, 8, TM], BF16)
        wsl = [sb(f"wsl{i}", [128, 12288], BF16) for i in range(2)]
        wgs = sb("wgs", [128, 8, 256], BF16)
        stg = [sb(f"stg{i}", [128, CH], BF16) for i in range(2)]
        kTh2 = [sb(f"kTh{i}", [128, S], BF16) for i in range(2)]
        vh2 = [sb(f"vh{i}", [128, S // 128, 128], BF16) for i in range(2)]
        qT2 = [sb(f"qT{i}", [128, TM], BF16) for i in range(2)]
        vst = [sb(f"vst{i}", [128, 1024], BF16) for i in range(2)]
        NTMP = 9
        TW = TT + 4
        Wf = [sb(f"Wf{i}", [128, 2 * TW], F32) for i in range(4)]
        tmp = []
        for i in range(4):
            tmp += [Wf[i][:, 0:TW], Wf[i][:, TW:2 * TW]]
        tmp.append(sb("tmp8", [128, TW], F32)[:])
        tmp9 = sb("tmp9", [128, TW], F32)[:]
        Bw = [sb(f"Bw{i}", [128, 2 * TT], BF16) for i in range(6)]
        tb16 = [Bw[i][:, 0:TT] for i in range(6)]
        RW = sb("RW", [128, 2 * TT], F32)
        msk = sb("msk", [128, 4, TT], BF16)
        nltri = sb("nltri", [128, 128], BF16)
        ones = sb("ones", [128, 128], BF16)
        prm = sb("prm_sb", [128, NPRM], F32)
        nc8 = sb("nc8", [128, 8], F32)
        hcar = sb("hcar", [128, 8], F32)
        xrcar = sb("xrcar", [128, 8, 3], F32)
        fcar = sb("fcar", [128, 2, 48, 2], F32)
        pw = [es.enter_context(nc.psum_tensor(f"pw{i}", [128, 2 * TT], F32)) for i in range(4)]
        ps = []
        for i in range(4):
            ps += [pw[i][:, 0:TT], pw[i][:, TT:2 * TT]]
        PSK = [("ps", i) for i in range(8)]
        print("sbuf bytes remaining:", nc.sbuf_bytes_remaining, flush=True)

        def stop_here(tag, mt=0):
            if STOP == tag and mt == STOP_MT:
                g0_ = mt * TM
                P.dma("pool", outT[:, :, g0_:g0_ + TM], xres[:], r=[("xres", m, t) for m in range(8) for t in range(NTT)],
                      w=[("out", mt)], stream="outst")
                P.finish()
                print("STOP at", tag, "ops:", P.nops, flush=True)
                raise _StopBuild()

        def gb(gsel):
            return Bw[2 + gsel // 2][:, (gsel % 2) * TT:(gsel % 2 + 1) * TT], f"tb{2 + gsel // 2}"

        def pcol(name, i):
            o = PRM[name] + i
            return prm[:, o:o + 1]

        P.dma("sp", prm[:], prm_d[:, :], w=["prm"], stream="misc")
        P.op("pool", lambda e: e.memset(hcar[:], 0.0), w=[("hcar", c_) for c_ in range(8)])
        P.op("pool", lambda e: e.memset(xrcar[:], 0.0), w=[("xrcar", c_) for c_ in range(8)])
        P.op("pool", lambda e: e.memset(fcar[:], 0.0), w=[("fcar", l_, q_) for l_ in range(2) for q_ in range(48)])
        P.op("pool", lambda e: e.memset(ones[:], 1.0), w=["ones"])
        io_i = tmp[7][:, 0:TT].bitcast(I32)
        P.op("pool", lambda e: e.iota(io_i, pattern=[[1, TT]], base=0, channel_multiplier=-1), w=["t7"])
        io_f = tmp[8][:, 0:TT]
        P.op("dve", lambda e: e.tensor_copy(out=io_f, in_=io_i), r=["t7"], w=["t8"])
        P.op("dve", lambda e: e.tensor_scalar(out=nltri[:], in0=io_f[:, 0:128], scalar1=0.0, scalar2=-1.0, op0=ALU.is_le, op1=ALU.mult),
             r=["t8"], w=["nltri"])
        for d in range(4):
            P.op("dve", lambda e, d=d: e.tensor_single_scalar(out=msk[:, d, :], in_=io_f, scalar=float(128 * d),
                                                          op=ALU.is_gt), r=["t8"], w=[("msk", d)])
        P.op("act", lambda e: e.activation(out=nc8[:], in_=prm[:, PRM["apar"]:PRM["apar"] + 8], func=AF.Exp, scale=-1.0),
             r=["prm"], w=["nc8"])
        P.op("act", lambda e: e.activation(out=nc8[:], in_=nc8[:], func=AF.Ln, bias=1.0), r=["nc8"], w=["nc8"])
        P.op("dve", lambda e: e.tensor_scalar(out=nc8[:], in0=nc8[:], scalar1=-8.0, scalar2=None, op0=ALU.mult),
             r=["nc8"], w=["nc8"])


        wchunks = {n_: [("wbf", c) for c in range(o_ // CH, (o_ + e_) // CH)] for n_, (o_, e_) in POFF.items()}
        cast_done = set()
        cast_ctr = [0]

        def cast_piece(name):
            if name in cast_done:
                return
            cast_done.add(name)
            o_, e_ = POFF[name]
            for ci in range(o_ // CH, (o_ + e_) // CH):
                s_ = cast_ctr[0] % 2
                cast_ctr[0] += 1
                P.dma("pool", stg[s_][:], wall[:, ci * CH:(ci + 1) * CH], w=[("stg", s_)], stream=f"cast{s_}")
                P.dma("pool", wbf[:, ci * CH:(ci + 1) * CH], stg[s_][:], r=[("stg", s_)], w=[("wbf", ci)], stream=f"wst{s_}")

        slot_ctr = [0]

        def load_w(name):
            cast_piece(name)
            o_, e_ = POFF[name]
            if name == "wg":
                P.dma("sp", wgs[:].rearrange("p a b -> p (a b)"), wbf[:, o_:o_ + e_], r=wchunks[name], w=["wgs"], stream="wg")
                return None
            s_ = slot_ctr[0] % 2
            slot_ctr[0] += 1
            P.dma("sp", wsl[s_][:, 0:e_], wbf[:, o_:o_ + e_], r=wchunks[name], w=[("wsl", s_)], stream=f"wl{s_}")
            return s_

        psrot = [0]

        def nps(lo=0, hi=8):
            i = lo + psrot[0] % (hi - lo)
            psrot[0] += 1
            return i

        def mm_group(pi, pairs, r, w_extra=()):
            n = len(pairs)
            h = None
            for k, (l, rr) in enumerate(pairs):
                h = P.op("pe", lambda e, l=l, rr=rr, k=k: e.matmul(ps[pi][:], lhsT=l, rhs=rr, start=(k == 0), stop=(k == n - 1)),
                         r=r, w=[PSK[pi]] + list(w_extra), inc=(k == n - 1))
            return h

        def layer_norm(t, lidx):
            tc_ = slice(t * TT, (t + 1) * TT)
            XR = [("xres", m, t) for m in range(8)]
            P.op("act", lambda e: e.activation(out=hy[:, :, tc_], in_=xres[:, :, tc_], func=AF.Square), r=XR, w=[("hy", t)])
            P.op("act", lambda e: e.activation(out=xbf[:, :, tc_], in_=xres[:, :, tc_], func=AF.Copy), r=XR, w=[("xbf", t)])
            p1, p2 = nps(), nps()
            mm_group(p1, [(ones[:], xbf[:, m, tc_]) for m in range(8)], r=["ones", ("xbf", t)])
            mm_group(p2, [(ones[:], hy[:, m, tc_]) for m in range(8)], r=["ones", ("hy", t)])
            mean, msq, rstd, tt_ = tmp[0], tmp[1], tmp[2], tmp[3]
            P.op("dve", lambda e: e.tensor_scalar(out=mean[:, 0:TT], in0=ps[p1][:], scalar1=1.0 / D, scalar2=None, op0=ALU.mult),
                 r=[PSK[p1]], w=["t0"])
            P.op("dve", lambda e: e.tensor_tensor(out=msq[:, 0:TT], in0=mean[:, 0:TT], in1=mean[:, 0:TT], op=ALU.mult),
                 r=["t0"], w=["t1"])
            P.op("dve", lambda e: e.scalar_tensor_tensor(out=msq[:, 0:TT], in0=ps[p2][:], scalar=1.0 / D, in1=msq[:, 0:TT],
                                                         op0=ALU.mult, op1=ALU.subtract), r=[PSK[p2], "t1"], w=["t1"])
            P.op("dve", lambda e: e.tensor_scalar(out=msq[:, 0:TT], in0=msq[:, 0:TT], scalar1=EPS, scalar2=None, op0=ALU.add),
                 r=["t1"], w=["t1"])
            P.op("act", lambda e: e.activation(out=rstd[:, 0:TT], in_=msq[:, 0:TT], func=AF.Sqrt), r=["t1"], w=["t2"])
            P.op("dve", lambda e: e.reciprocal(out=rstd[:, 0:TT], in_=rstd[:, 0:TT]), r=["t2"], w=["t2"])
            for m in range(8):
                tk = "t3" if m % 2 == 0 else "t4"
                tb_ = tmp[3] if m % 2 == 0 else tmp[4]
                P.op("dve", lambda e, m=m, tb_=tb_: e.tensor_tensor(out=tb_[:, 0:TT], in0=xres[:, m, tc_], in1=mean[:, 0:TT], op=ALU.subtract),
                     r=[("xres", m, t), "t0"], w=[tk])
                P.op("dve", lambda e, tb_=tb_: e.tensor_tensor(out=tb_[:, 0:TT], in0=tb_[:, 0:TT], in1=rstd[:, 0:TT], op=ALU.mult),
                     r=[tk, "t2"], w=[tk])
                P.op("act", lambda e, m=m, tb_=tb_: e.activation(out=xres[:, m, tc_], in_=tb_[:, 0:TT], func=AF.Identity,
                                                                scale=pcol("lng", lidx * 8 + m), bias=pcol("lnb", lidx * 8 + m)),
                     r=[tk, "prm"], w=[("xres", m, t)])
                P.op("act", lambda e, m=m, tb_=tb_: e.activation(out=xbf[:, m, tc_], in_=tb_[:, 0:TT], func=AF.Identity,
                                                                scale=pcol("lng", lidx * 8 + m), bias=pcol("lnb", lidx * 8 + m)),
                     r=[tk, "prm"], w=[("xbf", t)])

        def proj_res_ln(slot, src_key, bias_name, lidx):
            W = wsl[slot][:, 0:8192].rearrange("p (k n) -> p k n", k=8)
            for t in range(NTT):
                tc_ = slice(t * TT, (t + 1) * TT)
                for m in range(8):
                    pi = nps()
                    mm_group(pi, [(W[:, c, m * 128:(m + 1) * 128], hy[:, c, tc_]) for c in range(8)],
                             r=[("wsl", slot), (src_key, t)])
                    if bias_name is not None:
                        tk = "t5" if m % 2 == 0 else "t6"
                        tb_ = tmp[5] if m % 2 == 0 else tmp[6]
                        P.op("act", lambda e, m=m, tb_=tb_, pi=pi: e.activation(out=tb_[:, 0:TT], in_=ps[pi][:], func=AF.Identity,
                                                                               bias=pcol(bias_name, m)),
                             r=[PSK[pi], "prm"], w=[tk])
                        P.op("dve", lambda e, m=m, tb_=tb_: e.scalar_tensor_tensor(out=xres[:, m, tc_], in0=xres[:, m, tc_], scalar=ALPHA,
                                                                                 in1=tb_[:, 0:TT], op0=ALU.mult, op1=ALU.add),
                             r=[tk, ("xres", m, t)], w=[("xres", m, t)])
                    else:
                        P.op("dve", lambda e, m=m, pi=pi: e.scalar_tensor_tensor(out=xres[:, m, tc_], in0=xres[:, m, tc_], scalar=ALPHA,
                                                                               in1=ps[pi][:], op0=ALU.mult, op1=ALU.add),
                             r=[PSK[pi], ("xres", m, t)], w=[("xres", m, t)])
            for t in range(NTT):
                layer_norm(t, lidx)

        def ffn(layer, lidx):
            units = [(g, t) for g in range(6) for t in range(NTT)]
            slots = {}

            def up(g, t, js=range(4)):
                if g not in slots:
                    slots[g] = load_w(f"ffn{layer}_{g}")
                slot = slots[g]
                WU = wsl[slot][:, 0:8192].rearrange("p (k n) -> p k n", k=8)
                tc_ = slice(t * TT, (t + 1) * TT)
                for j in js:
                    res = []
                    for br in range(2):
                        q = br * 24 + g * 4 + j
                        pi = nps(0, 4)
                        mm_group(pi, [(WU[:, kc, br * 512 + j * 128: br * 512 + (j + 1) * 128], xbf[:, kc, tc_]) for kc in range(8)],
                                 r=[("wsl", slot), ("xbf", t)])
                        bi = 4 * (j % 2) + 2 * br
                        hb, cv = tmp[bi], tmp[bi + 1]
                        hk, ck = f"t{bi}", f"t{bi + 1}"
                        wo_ = PRM["fcw"] + (layer * 48 + q) * 3
                        bo_ = PRM["fcb"] + layer * 48 + q
                        P.op("act", lambda e, hb=hb, q=q: e.activation(out=hb[:, 0:2], in_=fcar[:, layer, q, :], func=AF.Copy), r=[("fcar", layer, q)], w=[hk])
                        P.op("act", lambda e, hb=hb, pi=pi: e.activation(out=hb[:, 2:TT + 2], in_=ps[pi][:], func=AF.Copy),
                             r=[PSK[pi]], w=[hk])
                        P.op("act", lambda e, cv=cv, pi=pi, wo_=wo_, bo_=bo_: e.activation(
                            out=cv[:, 0:TT], in_=ps[pi][:], func=AF.Identity, scale=prm[:, wo_ + 2:wo_ + 3], bias=prm[:, bo_:bo_ + 1]),
                            r=[PSK[pi], "prm"], w=[ck])
                        P.op("act", lambda e, hb=hb, q=q: e.activation(out=fcar[:, layer, q, :], in_=hb[:, TT:TT + 2], func=AF.Copy), r=[hk], w=[("fcar", layer, q)])
                        P.op("dve", lambda e, hb=hb, cv=cv, wo_=wo_: e.scalar_tensor_tensor(
                            out=cv[:, 0:TT], in0=hb[:, 1:TT + 1], scalar=prm[:, wo_ + 1:wo_ + 2], in1=cv[:, 0:TT], op0=ALU.mult, op1=ALU.add),
                            r=[hk, ck, "prm"], w=[ck])
                        P.op("dve", lambda e, hb=hb, cv=cv, wo_=wo_: e.scalar_tensor_tensor(
                            out=cv[:, 0:TT], in0=hb[:, 0:TT], scalar=prm[:, wo_:wo_ + 1], in1=cv[:, 0:TT], op0=ALU.mult, op1=ALU.add),
                            r=[hk, ck, "prm"], w=[ck])
                        res.append((cv, ck))
                    (ca, cak), (cb, cbk) = res
                    P.op("act", lambda e, ca=ca: e.activation(out=ca[:, 0:TT], in_=ca[:, 0:TT], func=AF.Gelu_apprx_tanh), r=[cak], w=[cak])
                    gsel = (t % 2) * 4 + j
                    gap, gkey = gb(gsel)
                    P.op("dve", lambda e, ca=ca, cb=cb, gap=gap: e.tensor_tensor(out=gap, in0=ca[:, 0:TT], in1=cb[:, 0:TT], op=ALU.mult),
                         r=[cak, cbk], w=[gkey])

            def down(g, t, ms=range(8)):
                slot = slots[g]
                WD = wsl[slot][:, 8192:12288].rearrange("p (k n) -> p k n", k=4)
                tc_ = slice(t * TT, (t + 1) * TT)
                for m in ms:
                    pi = nps(4, 8)
                    mm_group(pi, [(WD[:, j, m * 128:(m + 1) * 128], gb((t % 2) * 4 + j)[0]) for j in range(4)],
                             r=[("wsl", slot)] + [gb((t % 2) * 4 + j)[1] for j in range(4)])
                    if g == 0:
                        P.op("dve", lambda e, m=m, pi=pi: e.scalar_tensor_tensor(out=xres[:, m, tc_], in0=xres[:, m, tc_], scalar=ALPHA,
                                                                               in1=ps[pi][:], op0=ALU.mult, op1=ALU.add),
                             r=[PSK[pi], ("xres", m, t)], w=[("xres", m, t)])
                    else:
                        P.op("dve", lambda e, m=m, pi=pi: e.tensor_tensor(out=xres[:, m, tc_], in0=xres[:, m, tc_], in1=ps[pi][:], op=ALU.add),
                             r=[PSK[pi], ("xres", m, t)], w=[("xres", m, t)])

            for ui, (g, t) in enumerate(units):
                for j in range(4):
                    up(g, t, [j])
                    if ui > 0:
                        down(*units[ui - 1], ms=[2 * j, 2 * j + 1])
            down(*units[-1])
            for t in range(NTT):
                layer_norm(t, lidx)

        for mt in range(NMT):
            g0 = mt * TM
            XRALL = [("xres", m, t) for m in range(8) for t in range(NTT)]
            P.dma("pool", xres[:], xT[:, :, g0:g0 + TM], w=XRALL, stream="xin")
            if mt == 0:
                for n_, _e in PIECES:
                    cast_piece(n_)
            for t in range(NTT):
                tc_ = slice(t * TT, (t + 1) * TT)
                P.op("act", lambda e, tc_=tc_: e.activation(out=xbf[:, :, tc_], in_=xres[:, :, tc_], func=AF.Copy),
                     r=[("xres", m, t) for m in range(8)], w=[("xbf", t)])

            stop_here("xload", mt)
            sg = load_w("win_g")
            sr = load_w("win_r")
            load_w("wg")
            stop_here("cast", mt)
            WG = wsl[sg][:, 0:8192].rearrange("p (k n) -> p k n", k=8)
            WR = wsl[sr][:, 0:8192].rearrange("p (k n) -> p k n", k=8)
            gi, gr, a_, m2, u_, h_ = tmp[2:8]

            def geluG(t):
                tc_ = slice(t * TT, (t + 1) * TT)
                for c in range(8):
                    py = nps()
                    mm_group(py, [(WG[:, kc, c * 128:(c + 1) * 128], xbf[:, kc, tc_]) for kc in range(8)], r=[("wsl", sg), ("xbf", t)])
                    P.op("act", lambda e, c=c, py=py: e.activation(out=hy[:, c, tc_], in_=ps[py][:], func=AF.Gelu_apprx_tanh, bias=pcol("b_in", c)),
                         r=[PSK[py], "prm"], w=[("hy", t)])

            def bufsA(n):
                s_ = n % 2
                return ((tmp[0], "t0") if s_ == 0 else (tmp[8], "t8"), (tmp[1], "t1") if s_ == 0 else (tmp9, "t9"), (tb16[s_], f"tb{s_}"))

            def stageA(n):
                t, c = divmod(n, 8)
                tc_ = slice(t * TT, (t + 1) * TT)
                (xr, xrk), (xbr, xbrk), (xbrb, xbk) = bufsA(n)
                pr = nps()
                mm_group(pr, [(WR[:, kc, c * 128:(c + 1) * 128], xbf[:, kc, tc_]) for kc in range(8)], r=[("wsl", sr), ("xbf", t)])
                P.op("dve", lambda e: e.tensor_copy(out=xr[:, 0:3], in_=xrcar[:, c, :]), r=[("xrcar", c)], w=[xrk])
                P.op("act", lambda e: e.activation(out=xr[:, 3:TT + 3], in_=ps[pr][:], func=AF.Identity, bias=pcol("b_in", 8 + c)),
                     r=[PSK[pr], "prm"], w=[xrk])
                P.op("dve", lambda e: e.tensor_copy(out=xrcar[:, c, :], in_=xr[:, TT:TT + 3]), r=[xrk], w=[("xrcar", c)])
                lw = PRM["lcw"] + c * 4
                P.op("dve", lambda e: e.tensor_scalar(out=xbr[:, 0:TT], in0=xr[:, 3:TT + 3], scalar1=prm[:, lw + 3:lw + 4],
                                                      scalar2=pcol("lcb", c), op0=ALU.mult, op1=ALU.add), r=[xrk, "prm"], w=[xbrk])
                for k in (2, 1, 0):
                    P.op("dve", lambda e, k=k: e.scalar_tensor_tensor(out=xbr[:, 0:TT], in0=xr[:, k:TT + k], scalar=prm[:, lw + k:lw + k + 1],
                                                                     in1=xbr[:, 0:TT], op0=ALU.mult, op1=ALU.add), r=[xrk, xbrk, "prm"], w=[xbrk])
                P.op("act", lambda e: e.activation(out=xbrb, in_=xbr[:, 0:TT], func=AF.Copy), r=[xbrk], w=[xbk])

            setB = [
                dict(gi=(tmp[2], "t2"), gr=(tmp[3], "t3"), a=(tmp[4], "t4"), m2=(tmp[5], "t5"), u=(tmp[6], "t6"), h=(tmp[7], "t7")),
                dict(gi=(Bw[2][:].bitcast(F32), "tb2"), gr=(Bw[3][:].bitcast(F32), "tb3"), a=(Bw[4][:].bitcast(F32), "tb4"),
                     m2=(Bw[5][:].bitcast(F32), "tb5"), u=(RW[:, 0:TT], "RW"), h=(RW[:, TT:2 * TT], "RW")),
            ]

            def stageB(n):
                t, c = divmod(n, 8)
                tc_ = slice(t * TT, (t + 1) * TT)
                (xr, xrk), (xbr, xbrk), (xbrb, xbk) = bufsA(n)
                B_ = setB[n % 2]
                (gi, gik), (gr, grk), (a_, ak), (m2, m2k), (u_, uk), (h_, hk_) = B_["gi"], B_["gr"], B_["a"], B_["m2"], B_["u"], B_["h"]
                pgi, pgr = nps(), nps()
                mm_group(pgi, [(wgs[:, c, 0:128], xbrb)], r=["wgs", xbk])
                mm_group(pgr, [(wgs[:, c, 128:256], xbrb)], r=["wgs", xbk])
                yield
                P.op("act", lambda e: e.activation(out=gi[:, 0:TT], in_=ps[pgi][:], func=AF.Sigmoid, bias=pcol("bg", c)),
                     r=[PSK[pgi], "prm"], w=[gik])
                P.op("act", lambda e: e.activation(out=gr[:, 0:TT], in_=ps[pgr][:], func=AF.Sigmoid, bias=pcol("bg", 8 + c)),
                     r=[PSK[pgr], "prm"], w=[grk])
                yield
                P.op("act", lambda e: e.activation(out=a_[:, 0:TT], in_=gr[:, 0:TT], func=AF.Exp, scale=nc8[:, c:c + 1]),
                     r=[grk, "nc8"], w=[ak])
                yield
                P.op("dve", lambda e: e.scalar_tensor_tensor(out=m2[:, 0:TT], in0=a_[:, 0:TT], scalar=-1.0, in1=a_[:, 0:TT], op0=ALU.mult, op1=ALU.mult),
                     r=[ak], w=[m2k])
                P.op("dve", lambda e: e.tensor_scalar(out=m2[:, 0:TT], in0=m2[:, 0:TT], scalar1=1.0, scalar2=1e-30, op0=ALU.add, op1=ALU.max),
                     r=[m2k], w=[m2k])
                yield
                P.op("act", lambda e: e.activation(out=m2[:, 0:TT], in_=m2[:, 0:TT], func=AF.Sqrt), r=[m2k], w=[m2k])
                if mt == 0 and t == 0:
                    P.op("dve", lambda e: e.memset(m2[:, 0:1], 1.0), r=[m2k], w=[m2k])
                P.op("dve", lambda e: e.tensor_tensor(out=u_[:, 0:TT], in0=gi[:, 0:TT], in1=xbr[:, 0:TT], op=ALU.mult), r=[gik, xbrk], w=[uk])
                yield
                P.op("dve", lambda e: e.tensor_tensor(out=u_[:, 0:TT], in0=u_[:, 0:TT], in1=m2[:, 0:TT], op=ALU.mult), r=[uk, m2k], w=[uk])
                P.op("dve", lambda e: e.tensor_tensor_scan(out=h_[:, 0:TT], data0=a_[:, 0:TT], data1=u_[:, 0:TT], initial=hcar[:, c:c + 1],
                                                           op0=ALU.mult, op1=ALU.add), r=[ak, uk, ("hcar", c)], w=[hk_])
                yield
                P.op("dve", lambda e: e.tensor_copy(out=hcar[:, c:c + 1], in_=h_[:, TT - 1:TT]), r=[hk_], w=[("hcar", c)])
                P.op("dve", lambda e: e.tensor_tensor(out=hy[:, c, tc_], in0=h_[:, 0:TT], in1=hy[:, c, tc_], op=ALU.mult),
                     r=[hk_, ("hy", t)], w=[("hy", t)])
                yield

            nA = NTT * 8
            geluG(0)
            for n in range(0, nA, 2):
                if n % 8 == 0 and n > 0:
                    geluG(n // 8)
                stageA(n)
                stageA(n + 1)
                g0_, g1_ = stageB(n), stageB(n + 1)
                d0 = d1 = False
                while not (d0 and d1):
                    if not d0:
                        d0 = next(g0_, "end") == "end"
                    if not d1:
                        d1 = next(g1_, "end") == "end"
            stop_here("mixer", mt)
            so = load_w("wout")
            proj_res_ln(so, "hy", "b_out", 0)
            stop_here("ln0", mt)
            ffn(0, 1)
            stop_here("ffn0", mt)

            sk = load_w("wk")
            WK = wsl[sk][:, 0:8192].rearrange("p (k n) -> p k n", k=8)
            for t in range(NTT):
                tc_ = slice(t * TT, (t + 1) * TT)
                for h in range(8):
                    pi = nps()
                    mm_group(pi, [(WK[:, kc, h * 128:(h + 1) * 128], xbf[:, kc, tc_]) for kc in range(8)], r=[("wsl", sk), ("xbf", t)])
                    P.op("act", lambda e, h=h, pi=pi, tc_=tc_: e.activation(out=hy[:, h, tc_], in_=ps[pi][:], func=AF.Copy),
                         r=[PSK[pi]], w=[("hy", t)])
            P.dma("pool", kT_d.rearrange("h d s -> d h s")[:, :, g0:g0 + TM], hy[:], r=[("hy", t) for t in range(NTT)], w=[("kd", mt)], stream="kst")
            sv = load_w("wv")
            WV = wsl[sv][:, 0:8192].rearrange("p (k n) -> p k n", k=8)
            for tb in range(TM // 128):
                t = tb // 4
                vs = vst[tb % 2]
                for half in range(2):
                    pi = nps()
                    mm_group(pi, [(xbf[:, kc, tb * 128:(tb + 1) * 128], WV[:, kc, half * 512:(half + 1) * 512]) for kc in range(8)],
                             r=[("wsl", sv), ("xbf", t)])
                    P.op("act", lambda e, vs=vs, half=half, pi=pi: e.activation(out=vs[:, half * 512:(half + 1) * 512], in_=ps[pi][:], func=AF.Copy),
                         r=[PSK[pi]], w=[("vst", tb % 2)])
                P.dma("pool", v_d.rearrange("h s d -> s h d")[g0 + tb * 128: g0 + (tb + 1) * 128], vs[:].rearrange("p (h d) -> p h d", h=8),
                      r=[("vst", tb % 2)], w=[("vd", mt, tb)], stream=f"vst{tb % 2}")

            stop_here("kv", mt)
            sq = load_w("wq")
            WQ = wsl[sq][:, 0:8192].rearrange("p (k n) -> p k n", k=8)
            nk = g0 + TM
            qbA = g0 // 128
            its = []
            for h in range(8):
                for li, kb in enumerate(range(qbA + 7, -1, -1)):
                    half = kb >= qbA + 4
                    lo = TT if half else 0
                    masks = []
                    if half:
                        masks.append((TT, kb - qbA - 4))
                    elif kb >= qbA:
                        masks.append((0, kb - qbA))
                    its.append(dict(h=h, kb=kb, lo=lo, masks=masks, first1=(kb == qbA + 7), first0=(kb == qbA + 3), last=(kb == 0), li=li))
            setup_done = set()
            ZK = [[PSK[0], PSK[1]], [PSK[2], PSK[3]]]
            CK = [PSK[4], PSK[5]]

            def head_setup(h):
                if h in setup_done or h >= 8:
                    return
                setup_done.add(h)
                hb_ = h % 2
                for t in range(NTT):
                    tc_ = slice(t * TT, (t + 1) * TT)
                    pi = 4 + t
                    mm_group(pi, [(WQ[:, kc, h * 128:(h + 1) * 128], xbf[:, kc, tc_]) for kc in range(8)], r=[("wsl", sq), ("xbf", t)])
                    P.op("dve", lambda e, pi=pi, tc_=tc_, hb_=hb_: e.tensor_scalar(out=qT2[hb_][:, tc_], in0=ps[pi][:], scalar1=SCALE, scalar2=None,
                                                                                 op0=ALU.mult), r=[PSK[pi]], w=[("qT", hb_, t)])
                P.dma("pool", kTh2[hb_][:, 0:nk], kT_d[h, :, 0:nk], r=[("kd", m_) for m_ in range(mt + 1)], w=[("kTh", hb_)], stream=f"kld{hb_}")
                P.dma("pool", vh2[hb_][:, 0:nk // 128, :], v_d[h, 0:nk, :].rearrange("(kb p) d -> p kb d", p=128),
                      r=[("vd", m_, tb) for m_ in range(mt + 1) for tb in range(8)], w=[("vh", hb_)], stream=f"vld{hb_}")

            def halves(I):
                return [1] if I["lo"] else [0, 1]

            def S1(i, I):
                h, kb, lo = I["h"], I["kb"], I["lo"]
                hb_ = h % 2
                zb = i % 2
                eW, ekeys = Wf[zb], [f"t{2 * zb}", f"t{2 * zb + 1}"]
                for q in halves(I):
                    mm_group(2 * zb + q, [(kTh2[hb_][:, kb * 128:(kb + 1) * 128], qT2[hb_][:, q * TT:(q + 1) * TT])],
                             r=[("kTh", hb_), ("qT", hb_, q)])
                P.op("act", lambda e: e.activation(out=eW[:, lo:2 * TT], in_=pw[zb][:, lo:2 * TT], func=AF.Exp),
                     r=[ZK[zb][q] for q in halves(I)], w=ekeys)

            def S1b(i, I):
                lo = I["lo"]
                zb = i % 2
                eW, ekeys = Wf[zb], [f"t{2 * zb}", f"t{2 * zb + 1}"]
                spW, spk = Bw[i % 3], f"tb{i % 3}"
                P.op("act", lambda e: e.activation(out=spW[:, lo:2 * TT], in_=eW[:, lo:2 * TT], func=AF.Ln, bias=1.0), r=ekeys, w=[spk])
                for (ml, dd) in I["masks"]:
                    P.op("pool", lambda e, ml=ml, dd=dd: e.tensor_tensor(out=spW[:, ml:ml + TT], in0=spW[:, ml:ml + TT], in1=msk[:, dd, :], op=ALU.mult),
                         r=[spk, ("msk", dd)], w=[spk])

            def S2a(i, I):
                kb, lo = I["kb"], I["lo"]
                zb = i % 2
                spW, spk = Bw[i % 3], f"tb{i % 3}"
                tWb, tkeys = Wf[2 + zb], [f"t{4 + 2 * zb}", f"t{5 + 2 * zb}"]
                if I["first1"]:
                    P.op("dve", lambda e: e.memset(RW[:], 0.0), w=["RW"])
                for q in halves(I):
                    P.op("pe", lambda e, q=q: e.matmul(ps[2 * zb + q][:], lhsT=nltri[:], rhs=spW[:, q * TT:(q + 1) * TT], start=False, stop=True,
                                                       skip_group_check=True), r=["nltri", spk], w=[ZK[zb][q]], inc=True)
                if kb > 0:
                    for q in halves(I):
                        mm_group(4 + q, [(ones[:], spW[:, q * TT:(q + 1) * TT])], r=["ones", spk])
                P.op("dve", lambda e: e.tensor_tensor(out=tWb[:, lo:2 * TT], in0=pw[zb][:, lo:2 * TT], in1=RW[:, lo:2 * TT], op=ALU.subtract),
                     r=[ZK[zb][q] for q in halves(I)] + ["RW"], w=tkeys)
                if kb > 0:
                    P.op("dve", lambda e: e.tensor_tensor(out=RW[:, lo:2 * TT], in0=RW[:, lo:2 * TT], in1=pw[2][:, lo:2 * TT], op=ALU.add),
                         r=[CK[q] for q in halves(I)] + ["RW"], w=["RW"])

            def S2b(i, I):
                lo = I["lo"]
                zb = i % 2
                tWb, tkeys = Wf[2 + zb], [f"t{4 + 2 * zb}", f"t{5 + 2 * zb}"]
                wW, wk_ = Bw[3 + i % 3], f"tb{3 + i % 3}"
                P.op("act", lambda e: e.activation(out=wW[:, lo:2 * TT], in_=tWb[:, lo:2 * TT], func=AF.Exp), r=tkeys, w=[wk_])
                for (ml, dd) in I["masks"]:
                    P.op("pool", lambda e, ml=ml, dd=dd: e.tensor_tensor(out=wW[:, ml:ml + TT], in0=wW[:, ml:ml + TT], in1=msk[:, dd, :], op=ALU.mult),
                         r=[wk_, ("msk", dd)], w=[wk_])

            def S3(i, I):
                h, kb = I["h"], I["kb"]
                hb_ = h % 2
                wW, wk_ = Bw[3 + i % 3], f"tb{3 + i % 3}"
                for q in halves(I):
                    first = I["first1"] if q == 1 else I["first0"]
                    P.op("pe", lambda e, q=q, first=first: e.matmul(ps[6 + q][:], lhsT=vh2[hb_][:, kb, :], rhs=wW[:, q * TT:(q + 1) * TT],
                                                                   start=first, stop=I["last"]),
                         r=[("vh", hb_), wk_], w=[PSK[6 + q]], inc=True)
                if I["last"]:
                    for q in range(2):
                        P.op("dve", lambda e, q=q: e.tensor_copy(out=hy[:, h, q * TT:(q + 1) * TT], in_=ps[6 + q][:]),
                             r=[PSK[6 + q]], w=[("hy", q)])

            n_it = len(its)
            head_setup(0)
            for k in range(n_it + 3):
                if k < n_it:
                    I = its[k]
                    if I["li"] == 3:
                        head_setup(I["h"] + 1)
                    S1(k, I)
                if 0 <= k - 1 < n_it:
                    S2a(k - 1, its[k - 1])
                if 0 <= k - 2 < n_it:
                    S2b(k - 2, its[k - 2])
                if k < n_it:
                    S1b(k, its[k])
                if 0 <= k - 3 < n_it:
                    S3(k - 3, its[k - 3])
            stop_here("attn", mt)
            sw = load_w("wo")
            proj_res_ln(sw, "hy", None, 2)
            stop_here("ln2", mt)
            ffn(1, 3)
            P.dma("pool", outT[:, :, g0:g0 + TM], xres[:], r=XRALL, w=[("out", mt)], stream="outst")
        P.finish()
        print("ops per engine:", P.nops, "sem counts:", P.cnt, flush=True)
    return nc


def _host_layout(inputs):
    f = lambda a: np.asarray(a, dtype=np.float32)
    def kmaj(w):
        K = w.shape[0] // 128
        return np.ascontiguousarray(w.reshape(K, 128, w.shape[1]).transpose(1, 0, 2)).reshape(128, -1)
    pieces = {}
    w_in = f(inputs["lru_w_in"])[0]
    pieces["win_g"] = kmaj(w_in[:, :1024]); pieces["win_r"] = kmaj(w_in[:, 1024:])
    pieces["wg"] = np.ascontiguousarray(f(inputs["lru_w_gates"])[0].transpose(1, 0, 2)).reshape(128, -1)
    pieces["wout"] = kmaj(f(inputs["lru_w_out"])[0])
    kv = f(inputs["kv_w"])
    pieces["wk"] = kmaj(kv[:, :1024]); pieces["wv"] = kmaj(kv[:, 1024:])
    pieces["wq"] = kmaj(f(inputs["attn_w_q"])[0]); pieces["wo"] = kmaj(f(inputs["attn_w_out"])[0])
    for l in range(2):
        wu = f(inputs["ffn_w_up"])[l]; wd = f(inputs["ffn_w_down"])[l]
        for g in range(6):
            up = np.concatenate([wu[:, g * 512:(g + 1) * 512], wu[:, DFF + g * 512: DFF + (g + 1) * 512]], axis=1)
            pieces[f"ffn{l}_{g}"] = np.concatenate([kmaj(up), kmaj(wd[g * 512:(g + 1) * 512])], axis=1)
    wall = np.concatenate([pieces[n] for n, _ in PIECES], axis=1)
    assert wall.shape == (128, ETOT)
    prm = np.zeros((128, NPRM), np.float32)
    cm = lambda v: np.ascontiguousarray(v.reshape(-1, 128).T)
    prm[:, PRM["b_in"]:PRM["b_in"] + 16] = cm(f(inputs["lru_b_in"])[0])
    lcw = f(inputs["lru_conv_w"])[0]
    prm[:, PRM["lcw"]:PRM["lcw"] + 32] = lcw.reshape(4, 8, 128).transpose(2, 1, 0).reshape(128, 32)
    prm[:, PRM["lcb"]:PRM["lcb"] + 8] = cm(f(inputs["lru_conv_b"])[0])
    bg = f(inputs["lru_b_gates"])[0]
    prm[:, PRM["bg"]:PRM["bg"] + 8] = bg[:, :128].T
    prm[:, PRM["bg"] + 8:PRM["bg"] + 16] = bg[:, 128:].T
    prm[:, PRM["apar"]:PRM["apar"] + 8] = cm(f(inputs["lru_a_param"])[0])
    prm[:, PRM["b_out"]:PRM["b_out"] + 8] = cm(f(inputs["lru_b_out"])[0])
    fcw = f(inputs["ffn_conv_w"])
    prm[:, PRM["fcw"]:PRM["fcw"] + 288] = fcw.reshape(2, 3, 48, 128).transpose(3, 0, 2, 1).reshape(128, 288)
    fcb = f(inputs["ffn_conv_b"])
    prm[:, PRM["fcb"]:PRM["fcb"] + 96] = fcb.reshape(2, 48, 128).transpose(2, 0, 1).reshape(128, 96)
    prm[:, PRM["lng"]:PRM["lng"] + 32] = f(inputs["ln_g"]).reshape(4, 8, 128).transpose(2, 0, 1).reshape(128, 32)
    prm[:, PRM["lnb"]:PRM["lnb"] + 32] = f(inputs["ln_b"]).reshape(4, 8, 128).transpose(2, 0, 1).reshape(128, 32)
    x = f(inputs["x"])
    xTs = [np.ascontiguousarray(x[b].T.reshape(8, 128, S).transpose(1, 0, 2)) for b in range(4)]
    return wall, prm, xTs


_NC_CACHE = []


def kernel(**inputs):
    wall, prm, xTs = _host_layout(inputs)
    if not _NC_CACHE:
        _NC_CACHE.append(build())
    nc = _NC_CACHE[0]
    zx, zw, zp = np.zeros_like(xTs[0]), np.zeros_like(wall), np.zeros_like(prm)
    in_maps = []
    for c in range(NCORES):
        if c in WORK_CORES:
            in_maps.append({"xT": xTs[WORK_CORES.index(c)], "wall": wall, "prm": prm})
        else:
            in_maps.append({"xT": zx, "wall": zw, "prm": zp})
    res = run_bass_kernel_spmd(nc, in_maps, core_ids=list(range(NCORES)))
    out = np.empty((4, S, D), np.float32)
    for b in range(4):
        o = np.asarray(res.results[WORK_CORES[b]]["outT"], dtype=np.float32)
        out[b] = o.transpose(2, 1, 0).reshape(S, D)
    return out
```
